# Optimizing a Trainium2 kernel written in Bass

```python
import math
import jax, jax.numpy as jnp
from jax import lax
import numpy as np

D_MODEL = 2048
BATCH = 2
SEQ = 8192
DEPTH = 4

HEAD_DIM = 128
N_HEADS_DIFF = 6
N_HEADS_MOBA = 4
N_HEADS_DIL = 6
N_SELF_HEADS = N_HEADS_DIFF + N_HEADS_MOBA + N_HEADS_DIL
W_DIFF = N_HEADS_DIFF * HEAD_DIM
W_MOBA = N_HEADS_MOBA * HEAD_DIM
W_DIL = N_HEADS_DIL * HEAD_DIM
MIX_WIDTH = W_DIFF + W_MOBA + W_DIL
DIFF_QK_DIM = HEAD_DIM // 2
Q_BLOCK = 128
MOBA_BLOCK = 256
MOBA_TOPK = 3
MOBA_Q_CHUNK = 64
DIL_PAIRS = ((128, 1), (512, 4), (2048, 16))
DIL_BLOCK = 128
N_BUCKETS = 32
REL_MAX_DIST = 2048
N_MEM = 256
N_MEM_HEADS = 4
MEM_WIDTH = N_MEM_HEADS * HEAD_DIM
D_FF = 5632
CONV_WIDTH = 3
NORM_EPS = 1e-6
NEG_INF = -1e30

kernel_name = 'hybrid_diff_moba_dilated_trunk'


def rms_norm(x, g):
    xf = x.astype(jnp.float32)
    y = xf * lax.rsqrt(jnp.mean(xf * xf, axis=-1, keepdims=True) + NORM_EPS)
    return (y * g.astype(jnp.float32)).astype(x.dtype)


def t5_bucket(dist):
    n = jnp.maximum(dist, 0)
    max_exact = N_BUCKETS // 2
    nf = jnp.maximum(n, 1).astype(jnp.float32)
    log_ratio = jnp.log(nf / max_exact) / math.log(REL_MAX_DIST / max_exact)
    large = max_exact + (log_ratio * (N_BUCKETS - max_exact)).astype(jnp.int32)
    large = jnp.minimum(large, N_BUCKETS - 1)
    return jnp.where(n < max_exact, n, large)


def rel_bias(table, dist):
    return jnp.moveaxis(table.astype(jnp.float32)[t5_bucket(dist)], -1, 0)


def diff_attention(q, k, v, lam, sub_g, lam_init, table):
    B, S, H, _, dqk = q.shape
    nq = S // Q_BLOCK
    scale = dqk ** -0.5
    qb = q.reshape(B, nq, Q_BLOCK, H, 2, dqk).transpose(1, 0, 3, 4, 2, 5)
    kt = k.transpose(0, 2, 3, 1, 4)
    vt = v.transpose(0, 2, 1, 3)
    kpos = jnp.arange(S)

    def block(args):
        i, qi = args
        qpos = i * Q_BLOCK + jnp.arange(Q_BLOCK)
        dist = qpos[:, None] - kpos[None, :]
        s = jnp.einsum('bhcqd,bhckd->bhcqk', qi, kt).astype(jnp.float32) * scale
        s = s + rel_bias(table, dist)[None, :, None]
        s = jnp.where(dist >= 0, s, NEG_INF)
        p = jax.nn.softmax(s, axis=-1)
        a = p[:, :, 0] - lam * p[:, :, 1]
        return jnp.einsum('bhqk,bhkd->bhqd', a.astype(vt.dtype), vt)

    o = lax.map(block, (jnp.arange(nq), qb))
    o = o.transpose(1, 0, 3, 2, 4).reshape(B, S, H, -1)
    return rms_norm(o, sub_g) * (1.0 - lam_init)


def moba_attention(q, k, v, table):
    B, S, H, D = q.shape
    Sp = -(-S // MOBA_BLOCK) * MOBA_BLOCK
    nb = Sp // MOBA_BLOCK
    ksel = min(MOBA_TOPK, nb)
    C = MOBA_Q_CHUNK
    nc = Sp // C
    pad = ((0, 0), (0, Sp - S), (0, 0), (0, 0))
    qt = jnp.pad(q, pad).transpose(0, 2, 1, 3)
    kt = jnp.pad(k, pad).transpose(0, 2, 1, 3)
    vt = jnp.pad(v, pad).transpose(0, 2, 1, 3)
    kb = kt.reshape(B, H, nb, MOBA_BLOCK, D)
    vb = vt.reshape(B, H, nb, MOBA_BLOCK, D)
    kmean = jnp.mean(kb.astype(jnp.float32), axis=3)
    gate = jnp.einsum('bhsd,bhnd->bhsn', qt.astype(jnp.float32), kmean)
    qblk = jnp.arange(Sp) // MOBA_BLOCK
    past = jnp.arange(nb)[None, :] < qblk[:, None]
    gate = jnp.where(past, gate, NEG_INF)
    _, sel = lax.top_k(gate, ksel)
    sel_ok = sel < qblk[:, None]
    qc = qt.reshape(B, H, nc, C, D).transpose(2, 0, 1, 3, 4)
    selc = sel.reshape(B, H, nc, C, ksel).transpose(2, 0, 1, 3, 4)
    okc = sel_ok.reshape(B, H, nc, C, ksel).transpose(2, 0, 1, 3, 4)
    scale = D ** -0.5
    tab_t = table.astype(jnp.float32).T
    bidx = jnp.arange(B)[:, None, None]
    hidx = jnp.arange(H)[None, :, None]
    koff = jnp.arange(MOBA_BLOCK)

    def chunk(args):
        i, qi, si, oki = args
        qpos = i * C + jnp.arange(C)
        own = (i * C) // MOBA_BLOCK
        flat = si.reshape(B, H, C * ksel)
        kg = kb[bidx, hidx, flat].reshape(B, H, C, ksel, MOBA_BLOCK, D)
        vg = vb[bidx, hidx, flat].reshape(B, H, C, ksel, MOBA_BLOCK, D)
        s_sel = jnp.einsum('bhqd,bhqjkd->bhqjk', qi, kg).astype(jnp.float32) * scale
        dist_sel = qpos[None, None, :, None, None] - (si[..., None] * MOBA_BLOCK + koff)
        s_sel = s_sel + tab_t[jnp.arange(H)[None, :, None, None, None], t5_bucket(dist_sel)]
        s_sel = jnp.where(oki[..., None], s_sel, NEG_INF)
        ko = lax.dynamic_slice_in_dim(kt, own * MOBA_BLOCK, MOBA_BLOCK, axis=2)
        vo = lax.dynamic_slice_in_dim(vt, own * MOBA_BLOCK, MOBA_BLOCK, axis=2)
        dist_own = qpos[:, None] - (own * MOBA_BLOCK + koff)[None, :]
        s_own = jnp.einsum('bhqd,bhkd->bhqk', qi, ko).astype(jnp.float32) * scale
        s_own = s_own + rel_bias(table, dist_own)[None]
        s_own = jnp.where(dist_own >= 0, s_own, NEG_INF)
        s = jnp.concatenate([s_sel.reshape(B, H, C, ksel * MOBA_BLOCK), s_own], axis=-1)
        p = jax.nn.softmax(s, axis=-1).astype(vt.dtype)
        p_sel = p[..., :ksel * MOBA_BLOCK].reshape(B, H, C, ksel, MOBA_BLOCK)
        p_own = p[..., ksel * MOBA_BLOCK:]
        return (jnp.einsum('bhqjk,bhqjkd->bhqd', p_sel, vg)
                + jnp.einsum('bhqk,bhkd->bhqd', p_own, vo))

    o = lax.map(chunk, (jnp.arange(nc), qc, selc, okc))
    o = o.transpose(1, 0, 3, 2, 4).reshape(B, Sp, H, D)
    return o[:, :S]


def dilated_branch(q, k, v, table, window, dil):
    B, H, S, D = q.shape
    L = S // dil
    steps = window // dil
    nb = -(-L // DIL_BLOCK)
    Lp = nb * DIL_BLOCK

    def sub(t):
        return t.reshape(B, H, L, dil, D).transpose(0, 1, 3, 2, 4)

    def windowed(t):
        tp = jnp.pad(sub(t), ((0, 0), (0, 0), (0, 0), (DIL_BLOCK, Lp - L), (0, 0)))
        tp = tp.reshape(B, H, dil, nb + 1, DIL_BLOCK, D)
        return jnp.concatenate([tp[:, :, :, :-1], tp[:, :, :, 1:]], axis=4)

    qs = jnp.pad(sub(q), ((0, 0), (0, 0), (0, 0), (0, Lp - L), (0, 0))).reshape(B, H, dil, nb, DIL_BLOCK, D)
    kw = windowed(k)
    vw = windowed(v)
    qi = jnp.arange(DIL_BLOCK)
    ki = jnp.arange(2 * DIL_BLOCK) - DIL_BLOCK
    j = qi[:, None] - ki[None, :]
    kidx = jnp.arange(nb)[:, None, None] * DIL_BLOCK + ki[None, None, :]
    valid = (j >= 0) & (j <= steps) & (kidx >= 0)
    s = jnp.einsum('bhrnqd,bhrnkd->bhrnqk', qs, kw).astype(jnp.float32) * D ** -0.5
    s = s + rel_bias(table, j * dil)[None, :, None, None]
    s = jnp.where(valid, s, NEG_INF)
    lse = jax.nn.logsumexp(s, axis=-1)
    p = jnp.exp(s - lse[..., None]).astype(v.dtype)
    o = jnp.einsum('bhrnqk,bhrnkd->bhrnqd', p, vw)
    o = o.reshape(B, H, dil, Lp, D)[:, :, :, :L].transpose(0, 1, 3, 2, 4).reshape(B, H, S, D)
    lse = lse.reshape(B, H, dil, Lp)[:, :, :, :L].transpose(0, 1, 3, 2).reshape(B, H, S)
    return o, lse


def dilated_attention(q, k, v, table):
    qt, kt, vt = (t.transpose(0, 2, 1, 3) for t in (q, k, v))
    outs, lses = [], []
    for window, dil in DIL_PAIRS:
        o, lse = dilated_branch(qt, kt, vt, table, window, dil)
        outs.append(o)
        lses.append(lse)
    wts = jax.nn.softmax(jnp.stack(lses), axis=0)
    o = jnp.sum(wts[..., None] * jnp.stack(outs).astype(jnp.float32), axis=0)
    return o.astype(q.dtype).transpose(0, 2, 1, 3)


def memory_cross_attention(h, mem_n, w_q, w_kv, w_o):
    B, S, _ = h.shape
    q = (h @ w_q).reshape(B, S, N_MEM_HEADS, HEAD_DIM)
    kv = (mem_n @ w_kv).reshape(B, -1, 2, N_MEM_HEADS, HEAD_DIM)
    k, v = kv[:, :, 0], kv[:, :, 1]
    s = jnp.einsum('bshd,bmhd->bhsm', q, k).astype(jnp.float32) * HEAD_DIM ** -0.5
    p = jax.nn.softmax(s, axis=-1).astype(v.dtype)
    o = jnp.einsum('bhsm,bmhd->bshd', p, v).reshape(B, S, MEM_WIDTH)
    return o @ w_o


def conv_glu_ffn(h, w_up, conv_w, conv_b, w_down):
    gu = h @ w_up
    g, u = jnp.split(gu, 2, axis=-1)
    g = lax.conv_general_dilated(
        g, conv_w[:, None, :].astype(g.dtype), window_strides=(1,),
        padding=[(CONV_WIDTH - 1, 0)], dimension_numbers=('NWC', 'WIO', 'NWC'),
        feature_group_count=D_FF) + conv_b
    return (jax.nn.silu(g) * u) @ w_down


def setup_inputs(seed: int = 0) -> dict:
    key = jax.random.key(seed)
    ks = jax.random.split(key, 19)
    f32 = jnp.float32

    def w(k, shape, fan_in):
        return jax.random.normal(k, shape, f32) * fan_in ** -0.5

    def gain(k, shape):
        return 1.0 + 0.02 * jax.random.normal(k, shape, f32)

    return {
        'x': jax.random.normal(ks[0], (BATCH, SEQ, D_MODEL), f32),
        'mem': jax.random.normal(ks[1], (BATCH, N_MEM, D_MODEL), f32),
        'w_in': w(ks[2], (DEPTH, D_MODEL, 3 * MIX_WIDTH), D_MODEL),
        'w_out': w(ks[3], (DEPTH, MIX_WIDTH, D_MODEL), MIX_WIDTH),
        'g_mix': gain(ks[4], (DEPTH, D_MODEL)),
        'diff_lambda': 0.1 * jax.random.normal(ks[5], (DEPTH, 4, DIFF_QK_DIM), f32),
        'diff_subln': gain(ks[6], (DEPTH, HEAD_DIM)),
        'rel_bias_table': 0.3 * jax.random.normal(ks[7], (N_BUCKETS, N_SELF_HEADS), f32),
        'g_cross': gain(ks[8], (DEPTH, D_MODEL)),
        'g_mem': gain(ks[9], (DEPTH, D_MODEL)),
        'w_cq': w(ks[10], (DEPTH, D_MODEL, MEM_WIDTH), D_MODEL),
        'w_ckv': w(ks[11], (DEPTH, D_MODEL, 2 * MEM_WIDTH), D_MODEL),
        'w_co': w(ks[12], (DEPTH, MEM_WIDTH, D_MODEL), MEM_WIDTH),
        'g_ffn': gain(ks[13], (DEPTH, D_MODEL)),
        'w_up': w(ks[14], (DEPTH, D_MODEL, 2 * D_FF), D_MODEL),
        'conv_w': w(ks[15], (DEPTH, CONV_WIDTH, D_FF), CONV_WIDTH),
        'conv_b': 0.01 * jax.random.normal(ks[16], (DEPTH, D_FF), f32),
        'w_down': w(ks[17], (DEPTH, D_FF, D_MODEL), D_FF),
        'g_final': gain(ks[18], (D_MODEL,)),
    }


def reference(x, mem, w_in, w_out, g_mix, diff_lambda, diff_subln, rel_bias_table,
              g_cross, g_mem, w_cq, w_ckv, w_co, g_ffn, w_up, conv_w, conv_b, w_down, g_final):
    B, S, _ = x.shape
    tab_a = rel_bias_table[:, :N_HEADS_DIFF]
    tab_b = rel_bias_table[:, N_HEADS_DIFF:N_HEADS_DIFF + N_HEADS_MOBA]
    tab_c = rel_bias_table[:, N_HEADS_DIFF + N_HEADS_MOBA:]
    widths = (W_DIFF,) * 3 + (W_MOBA,) * 3 + (W_DIL,) * 3
    cuts = [int(c) for c in np.cumsum(widths)[:-1]]
    for l in range(DEPTH):
        h = rms_norm(x, g_mix[l])
        qa, ka, va, qb, kb, vb, qc, kc, vc = jnp.split(h @ w_in[l], cuts, axis=-1)
        lam_init = 0.8 - 0.6 * math.exp(-0.3 * l)
        lp = diff_lambda[l].astype(jnp.float32)
        lam = jnp.exp(jnp.sum(lp[0] * lp[1])) - jnp.exp(jnp.sum(lp[2] * lp[3])) + lam_init
        o_a = diff_attention(qa.reshape(B, S, N_HEADS_DIFF, 2, DIFF_QK_DIM),
                             ka.reshape(B, S, N_HEADS_DIFF, 2, DIFF_QK_DIM),
                             va.reshape(B, S, N_HEADS_DIFF, HEAD_DIM),
                             lam, diff_subln[l], lam_init, tab_a)
        o_b = moba_attention(qb.reshape(B, S, N_HEADS_MOBA, HEAD_DIM),
                             kb.reshape(B, S, N_HEADS_MOBA, HEAD_DIM),
                             vb.reshape(B, S, N_HEADS_MOBA, HEAD_DIM), tab_b)
        o_c = dilated_attention(qc.reshape(B, S, N_HEADS_DIL, HEAD_DIM),
                                kc.reshape(B, S, N_HEADS_DIL, HEAD_DIM),
                                vc.reshape(B, S, N_HEADS_DIL, HEAD_DIM), tab_c)
        mixed = jnp.concatenate([o_a.reshape(B, S, W_DIFF), o_b.reshape(B, S, W_MOBA),
                                 o_c.reshape(B, S, W_DIL)], axis=-1)
        x = x + mixed @ w_out[l]
        x = x + memory_cross_attention(rms_norm(x, g_cross[l]), rms_norm(mem, g_mem[l]),
                                       w_cq[l], w_ckv[l], w_co[l])
        x = x + conv_glu_ffn(rms_norm(x, g_ffn[l]), w_up[l], conv_w[l], conv_b[l], w_down[l])
    return rms_norm(x, g_final)
```

```python
import math
import numpy as np
import ml_dtypes
from contextlib import ExitStack
import concourse.bass as bass
import concourse.mybir as mybir
from concourse.bass_utils import run_bass_kernel_spmd

F32 = mybir.dt.float32
BF16 = mybir.dt.bfloat16
AF = mybir.ActivationFunctionType
ALU = mybir.AluOpType
AX = mybir.AxisListType
NPBF = ml_dtypes.bfloat16


import types


def _freeze(fn):
    if fn.__closure__ is None:
        return fn
    cells = []
    for c in fn.__closure__:
        try:
            cells.append(types.CellType(c.cell_contents))
        except ValueError:
            cells.append(c)
    return types.FunctionType(fn.__code__, fn.__globals__, fn.__name__, fn.__defaults__, tuple(cells))


class Res:
    __slots__ = ("w", "r", "name")

    def __init__(self, name=""):
        self.w = None
        self.r = {}
        self.name = name


class Ev:
    __slots__ = ("key", "sem", "val", "slot")

    def __init__(self, key, sem, val, slot=None):
        self.key = key
        self.sem = sem
        self.val = val
        self.slot = slot

    def resolve(self):
        if self.slot is not None:
            return self.slot.key, self.slot.sem, self.slot.total
        return self.key, self.sem, self.val


class DmaSlot:
    _n = 0

    def __init__(self, kb, name):
        DmaSlot._n += 1
        self.key = "dma%d_%s" % (DmaSlot._n, name)
        self.sem = kb.stack.enter_context(kb.nc.semaphore(self.key))
        self.total = 0


class KB:
    CE = ("pe", "act", "dve", "pool")
    ROLL = 30000

    def __init__(self, nc, stack):
        self.nc = nc
        self.stack = stack
        self.prog = {e: [] for e in self.CE + ("sp",)}
        self.semgen = {e: 0 for e in self.CE}
        self.sem = {}
        self.semkey = {}
        self.cnt = {e: 0 for e in self.CE}
        for e in self.CE:
            self._newsem(e)
        self.waited = {e: {} for e in self.CE + ("sp",)}
        self.ninst = 0

    def _newsem(self, e):
        self.semgen[e] += 1
        self.semkey[e] = "%s%d" % (e, self.semgen[e])
        self.sem[e] = self.stack.enter_context(self.nc.semaphore("s_" + self.semkey[e]))
        self.cnt[e] = 0

    def slot(self, name):
        return DmaSlot(self, name)

    def _deps(self, eng, reads, writes):
        evs = []
        for R in reads:
            if R.w is not None:
                evs.append(R.w)
        for R in writes:
            if R.w is not None:
                evs.append(R.w)
            evs.extend(R.r.values())
        for ev in evs:
            key, sem, val = ev.resolve()
            if eng == "pe" and key.startswith("pe"):
                continue
            if self.waited[eng].get(key, 0) >= val:
                continue
            self.waited[eng][key] = val
            self.prog[eng].append(("wait", sem, val))

    def _mark(self, ev, rkey, reads, writes):
        for R in reads:
            R.r[rkey] = ev
        for R in writes:
            R.w = ev
            R.r = {}

    def op(self, eng, fn, reads=(), writes=()):
        fn = _freeze(fn)
        self._deps(eng, reads, writes)
        if self.cnt[eng] >= self.ROLL:
            self._newsem(eng)
        self.cnt[eng] += 1
        sem = self.sem[eng]
        ev = Ev(self.semkey[eng], sem, self.cnt[eng])
        self.prog[eng].append(("op", fn, sem))
        self._mark(ev, eng, reads, writes)
        self.ninst += 1
        return ev

    def dma(self, q, slot, out, in_, reads=(), writes=(), **kw):
        self._deps(q, reads, writes)
        slot.total += 16
        self.prog[q].append(("dma", out, in_, slot.sem, kw))
        ev = Ev(None, None, None, slot)
        self._mark(ev, slot.key, reads, writes)
        self.ninst += 1
        return ev

    def wait_slot(self, eng, slot):
        if self.waited[eng].get(slot.key, 0) >= slot.total:
            return
        self.waited[eng][slot.key] = slot.total
        self.prog[eng].append(("wait", slot.sem, slot.total))

    def emit(self):
        nc = self.nc
        with nc.Block() as block:
            def run(e, prog):
                for it in prog:
                    if it[0] == "wait":
                        e.wait_ge(it[1], it[2])
                    elif it[0] == "op":
                        it[1](e).then_inc(it[2], 1)
                    else:
                        e.dma_start(out=it[1], in_=it[2], **it[4]).then_inc(it[3], 16)

            @block.tensor
            def _(e):
                run(e, self.prog["pe"])

            @block.scalar
            def _(e):
                run(e, self.prog["act"])

            @block.vector
            def _(e):
                run(e, self.prog["dve"])

            @block.gpsimd
            def _(e):
                run(e, self.prog["pool"])

            @block.sync
            def _(e):
                run(e, self.prog["sp"])

import numpy as np, math
import ml_dtypes
NPBF = ml_dtypes.bfloat16
NEG = -30000.0

def t5_bucket_np(d):
    n = np.maximum(d, 0)
    nf = np.maximum(n, 1).astype(np.float32)
    lr = np.log(nf / np.float32(16)) / np.float32(math.log(2048 / 16))
    large = 16 + (lr * np.float32(16)).astype(np.int32)
    large = np.minimum(large, 31)
    return np.where(n < 16, n, large)

def dist_tiles(j):
    ki = np.arange(128)[:, None, None, None]
    de = np.arange(6)[None, :, None, None]
    ii = np.arange(4)[None, None, :, None]
    qj = np.arange(128)[None, None, None, :]
    s = 4 * de + 3 - ii
    dt = s - 3 + j
    d = 128 * dt + qj - ki
    far = np.broadcast_to(de == 5, d.shape)
    return d, far

def bias_tables(table, j):
    d, far = dist_tiles(j)
    bk = np.where(far, 31, t5_bucket_np(d))
    bt = table[bk]
    return np.ascontiguousarray(np.moveaxis(bt, -1, 0).reshape(16, 128, 3072)).astype(np.float32)

def const_masks(j):
    d, far = dist_tiles(j)
    causal = np.where(far | (d >= 0), 0.0, NEG)
    mult = ((d >= 0) & (d <= 128)).astype(np.int32) + ((d >= 0) & (d <= 512) & (d % 4 == 0)) + ((d >= 0) & (d <= 2048) & (d % 16 == 0))
    dil = np.where(mult > 0, np.log(np.maximum(mult, 1).astype(np.float64)), NEG)
    dil = np.where(far, NEG, dil)
    return np.stack([causal.reshape(128, 3072), dil.reshape(128, 3072)]).astype(np.float32)

def moba_consts(j):
    out = np.zeros((16, 96), np.float32)
    for m in range(16):
        t = 4 * m + j
        own = t // 2
        n = np.arange(32)
        past = n < own
        out[m, 0:32] = np.where(past, 0.0, -1e30)
        out[m, 32:64] = past
        out[m, 64:96] = (n == own)
    return out.reshape(1, -1)

def esel_const():
    e = np.zeros((32, 32, 128), np.float32)
    for n in range(32):
        e[n, n, :] = 1.0
    return e.reshape(32, -1).astype(NPBF)


D = 2048
NT = 16
TOK = NT * 128
EPS = 1e-6

def group_info():
    segs = [("q", 0, 6), ("k", 0, 6), ("v", 0, 6), ("q", 6, 4), ("k", 6, 4), ("v", 6, 4),
            ("q", 10, 6), ("k", 10, 6), ("v", 10, 6)]
    out = []
    for kind, h0, n in segs:
        for i in range(n):
            out.append((kind, h0 + i))
    return out

def rmsnorm_tile(kb, nc, xt, rx, gt, rg, ht, rh, junk, rj, small, rs, D=2048):
    ss = small[:, 0:1]; ms = small[:, 1:2]; sd = small[:, 2:3]; rstd = small[:, 3:4]
    kb.op("act", lambda e: e.activation(junk[:], xt[:], AF.Square, accum_out=ss), reads=[rx], writes=[rj, rs])
    kb.op("dve", lambda e: e.tensor_scalar(ms, ss, 1.0 / D, EPS, ALU.mult, ALU.add), reads=[rs], writes=[rs])
    kb.op("act", lambda e: e.activation(sd, ms, AF.Sqrt), reads=[rs], writes=[rs])
    kb.op("dve", lambda e: e.reciprocal(rstd, sd), reads=[rs], writes=[rs])
    kb.op("dve", lambda e: e.scalar_tensor_tensor(ht[:], xt[:], rstd, gt[:], ALU.mult, ALU.mult),
          reads=[rx, rs, rg], writes=[rh])

def build_A():
    nc = bass.Bass("TRN2", target_bir_lowering=False)
    x = nc.dram_tensor("x", [TOK, D], F32, kind="ExternalInput").ap()
    w_in = nc.dram_tensor("w_in", [D, 6144], F32, kind="ExternalInput").ap()
    g_mix = nc.dram_tensor("g_mix", [1, D], F32, kind="ExternalInput").ap()
    ident_d = nc.dram_tensor("ident", [128, 128], BF16, kind="ExternalInput").ap()
    qT = nc.dram_tensor("qT", [16, 128, TOK], BF16, kind="ExternalOutput").ap()
    kT = nc.dram_tensor("kT", [16, 128, TOK], BF16, kind="ExternalOutput").ap()
    vT = nc.dram_tensor("vT", [16, 128, TOK], BF16, kind="ExternalOutput").ap()
    with ExitStack() as st:
        kb = KB(nc, st)
        sb = lambda n, s, d: st.enter_context(nc.sbuf_tensor("s_" + n, s, d))
        ps = lambda n, s, d: st.enter_context(nc.psum_tensor("p_" + n, s, d))
        hT = sb("hT", [128, 16, TOK], BF16); r_hT = [Res() for _ in range(NT)]
        gt = sb("gt", [128, D], F32); rg = Res()
        ident = sb("ident_s", [128, 128], BF16); rid = Res()
        xt = [sb("xt%d" % i, [128, D], F32) for i in range(2)]; rx = [Res(), Res()]
        ht = [sb("ht%d" % i, [128, D], BF16) for i in range(2)]; rh = [Res(), Res()]
        junk = sb("junk", [128, D], BF16); rj = Res()
        small = [sb("small%d" % i, [128, 4], F32) for i in range(2)]; rs = [Res(), Res()]
        wb = [sb("wb%d" % i, [128, 16, 512], BF16) for i in range(2)]; rw = [Res(), Res()]
        stg = [sb("stg%d" % i, [128, TOK], BF16) for i in range(2)]; rstg = [Res(), Res()]
        vst = [sb("vst%d" % i, [128, 512], BF16) for i in range(2)]; rvst = [Res(), Res()]
        tp = [ps("tp%d" % i, [128, 512], BF16) for i in range(2)]; rtp = [Res(), Res()]
        mm = [ps("mm%d" % i, [128, 512], F32) for i in range(4)]; rmm = [Res() for _ in range(4)]
        s_c = kb.slot("const"); s_x = [kb.slot("x0"), kb.slot("x1")]; s_w = [kb.slot("w0"), kb.slot("w1")]
        s_o = [kb.slot("o0"), kb.slot("o1")]; s_v = [kb.slot("v0"), kb.slot("v1")]
        kb.dma("sp", s_c, gt[:], g_mix.broadcast_to([128, D]), writes=[rg])
        kb.dma("sp", s_c, ident[:], ident_d, writes=[rid])
        for m in range(NT):
            b = m % 2
            kb.dma("sp", s_x[b], xt[b][:], x[m * 128:(m + 1) * 128, :], writes=[rx[b]])
            rmsnorm_tile(kb, nc, xt[b], rx[b], gt, rg, ht[b], rh[b], junk, rj, small[b], rs[b])
            for c4 in range(4):
                pb = (m * 4 + c4) % 2
                for i in range(4):
                    c = c4 * 4 + i
                    kb.op("pe", lambda e, pb=pb, i=i, c=c, b=b: e.transpose(tp[pb][:, i * 128:(i + 1) * 128], ht[b][:, c * 128:(c + 1) * 128], ident[:]),
                          reads=[rh[b], rid], writes=[rtp[pb]])
                eng = "act" if c4 % 2 == 0 else "dve"
                src = tp[pb][:].rearrange("p (c t) -> p c t", c=4)
                dst = hT[:, c4 * 4:(c4 + 1) * 4, m * 128:(m + 1) * 128]
                if eng == "act":
                    kb.op("act", lambda e, src=src, dst=dst: e.activation(dst, src, AF.Copy), reads=[rtp[pb]], writes=[r_hT[m]])
                else:
                    kb.op("dve", lambda e, src=src, dst=dst: e.tensor_copy(dst, src), reads=[rtp[pb]], writes=[r_hT[m]])
        gi = group_info()
        w_v = w_in.rearrange("(kc kp) n -> kp kc n", kp=128)
        nmm = 0; nst = 0; nvs = 0
        for cb in range(12):
            b = cb % 2
            for q4 in range(4):
                kb.dma("pool", s_w[b], wb[b][:, q4 * 4:(q4 + 1) * 4, :], w_v[:, q4 * 4:(q4 + 1) * 4, cb * 512:(cb + 1) * 512], writes=[rw[b]])
            g = 0
            while g < 4:
                kind, h = gi[cb * 4 + g]
                if True:
                    sc = 1.0
                    if kind == "q":
                        sc = 64 ** -0.5 if h < 6 else 128 ** -0.5
                    sbi = nst % 2; nst += 1
                    for tc in range(4):
                        pi = nmm % 4; nmm += 1
                        for kc in range(16):
                            kb.op("pe", lambda e, pi=pi, b=b, kc=kc, g=g, tc=tc: e.matmul(mm[pi][:], wb[b][:, kc, g * 128:(g + 1) * 128], hT[:, kc, tc * 512:(tc + 1) * 512], start=(kc == 0), stop=(kc == 15)),
                                  reads=[rw[b]] + r_hT[tc * 4:(tc + 1) * 4], writes=[rmm[pi]])
                        dst = stg[sbi][:, tc * 512:(tc + 1) * 512]
                        if tc % 2 == 0:
                            kb.op("act", lambda e, dst=dst, pi=pi, sc=sc: e.activation(dst, mm[pi][:], AF.Copy, scale=sc), reads=[rmm[pi]], writes=[rstg[sbi]])
                        else:
                            kb.op("dve", lambda e, dst=dst, pi=pi, sc=sc: e.tensor_scalar(dst, mm[pi][:], sc, None, ALU.mult), reads=[rmm[pi]], writes=[rstg[sbi]])
                    outT = qT if kind == "q" else (kT if kind == "k" else vT)
                    kb.dma("sp", s_o[sbi], outT[h], stg[sbi][:], reads=[rstg[sbi]])
                    g += 1
        for s in s_o:
            kb.wait_slot("sp", s)
        kb.emit()
    return nc


S = 8192
NT = 16
TOK = 2048
NEG = -30000.0
NG = 6
BW = NG * 512

def head_type(h):
    return "diff" if h < 6 else ("moba" if h < 10 else "dil")

def build_B(heads=range(16)):
    nc = bass.Bass("TRN2", target_bir_lowering=False)
    din = lambda n, s, d: nc.dram_tensor(n, s, d, kind="ExternalInput").ap()
    qT = din("qT", [16, 128, TOK], BF16)
    kT = din("kT", [16, 128, S], BF16)
    va_d = din("va", [16, 128, 64 * 130], BF16)
    bt_d = din("bt", [16, 128, BW], F32)
    cm_d = din("cm", [2, 128, BW], F32)
    pm_d = din("pm", [1, 16 * 96], F32)
    esel_d = din("esel", [32, 32 * 128], BF16)
    ident_d = din("identf", [128, 128], F32)
    dl_d = din("dl", [1, 256], F32)
    sg_d = din("subln", [1, 128], F32)
    li_d = din("laminit", [1, 1], F32)
    out_d = nc.dram_tensor("mixed", [16, 128, NT * 128], BF16, kind="ExternalOutput").ap()
    with ExitStack() as st:
        kb = KB(nc, st)
        sb = lambda n, s, d: st.enter_context(nc.sbuf_tensor("s_" + n, s, d))
        ps = lambda n, s, d: st.enter_context(nc.psum_tensor("p_" + n, s, d))
        kt = [sb("kt%d" % i, [128, S], BF16) for i in range(2)]
        va = [sb("va%d" % i, [128, 64, 130], BF16) for i in range(2)]
        qt = [sb("qt%d" % i, [128, TOK], BF16) for i in range(2)]
        bias = [sb("bias%d" % i, [128, BW], F32) for i in range(2)]
        rhd = [Res(), Res()]
        s_hd = [kb.slot("hd0"), kb.slot("hd1")]
        cm = sb("cm", [128, 2, BW], F32); rc = Res()
        pm = sb("pm", [128, 16, 96], F32)
        esel = sb("esel", [32, 32, 128], BF16)
        identf = sb("identf", [128, 128], F32)
        dl = sb("dl", [128, 256], F32)
        sg = sb("sg", [128, 128], F32)
        li = sb("li", [128, 1], F32)
        sm = sb("sm", [128, 16], F32); rsm = Res()
        tmp = [sb("tmp%d" % i, [128, 512], F32) for i in range(3)]; rtmp = [Res() for _ in range(3)]
        pT = [sb("pT%d" % i, [128, 512], BF16) for i in range(3)]; rpT = [Res() for _ in range(3)]
        ost = [sb("ost%d" % i, [128, NT, 128], BF16) for i in range(2)]; rost = [Res(), Res()]
        s_ost = [kb.slot("ost0"), kb.slot("ost1")]
        of = [sb("of%d" % i, [128, 128], F32) for i in range(2)]; rof = [Res(), Res()]
        junk = sb("junk", [128, 128], F32); rjunk = Res()
        rs = [sb("rs%d" % i, [128, 8], F32) for i in range(2)]; rrs = [Res(), Res()]
        km = sb("km", [128, 32], F32); kmh = sb("kmh", [128, 32], BF16); kml = sb("kml", [128, 32], BF16); rkm = Res()
        gm = [sb("gm%d" % i, [128, 32], F32) for i in range(2)]; rgm = [Res(), Res()]
        m8 = [sb("m8%d" % i, [128, 8], F32) for i in range(2)]
        selT = [sb("selT%d" % i, [32, 128], BF16) for i in range(2)]; rselT = [Res(), Res()]
        sps = [ps("sps%d" % i, [128, 512], F32) for i in range(3)]; rsps = [Res() for _ in range(3)]
        ops_ = [ps("ops%d" % i, [128, 512], F32) for i in range(4)]; rops = [Res() for _ in range(4)]
        gps = ps("gps", [128, 512], F32); rgps = Res()
        s_c = kb.slot("const")
        kb.dma("sp", s_c, cm[:], cm_d.rearrange("t p w -> p t w"), writes=[rc])
        kb.dma("sp", s_c, pm[:], pm_d.broadcast_to([128, 16 * 96]).rearrange("p (m w) -> p m w", m=16), writes=[rc])
        kb.dma("sp", s_c, esel[:], esel_d.rearrange("r (n c) -> r n c", n=32), writes=[rc])
        kb.dma("sp", s_c, identf[:], ident_d, writes=[rc])
        kb.dma("sp", s_c, dl[:], dl_d.broadcast_to([128, 256]), writes=[rc])
        kb.dma("sp", s_c, sg[:], sg_d.broadcast_to([128, 128]), writes=[rc])
        kb.dma("sp", s_c, li[:], li_d.broadcast_to([128, 1]), writes=[rc])
        V = "dve"
        kb.op(V, lambda e: e.tensor_tensor(junk[:, 0:64], dl[:, 0:64], dl[:, 64:128], ALU.mult), reads=[rc], writes=[rjunk])
        kb.op(V, lambda e: e.tensor_reduce(sm[:, 0:1], junk[:, 0:64], AX.X, ALU.add), reads=[rjunk], writes=[rsm])
        kb.op(V, lambda e: e.tensor_tensor(junk[:, 64:128], dl[:, 128:192], dl[:, 192:256], ALU.mult), reads=[rc], writes=[rjunk])
        kb.op(V, lambda e: e.tensor_reduce(sm[:, 1:2], junk[:, 64:128], AX.X, ALU.add), reads=[rjunk], writes=[rsm])
        kb.op("act", lambda e: e.activation(sm[:, 2:4], sm[:, 0:2], AF.Exp), reads=[rsm], writes=[rsm])
        kb.op(V, lambda e: e.tensor_tensor(sm[:, 4:5], sm[:, 3:4], sm[:, 2:3], ALU.subtract), reads=[rsm], writes=[rsm])
        kb.op(V, lambda e: e.tensor_tensor(sm[:, 4:5], sm[:, 4:5], li[:, 0:1], ALU.subtract), reads=[rsm, rc], writes=[rsm])
        kb.op(V, lambda e: e.tensor_scalar(sm[:, 5:6], li[:, 0:1], -1.0, 1.0, ALU.mult, ALU.add), reads=[rc], writes=[rsm])
        kb.op(V, lambda e: e.tensor_scalar(sg[:], sg[:], sm[:, 5:6], None, ALU.mult), reads=[rsm, rc], writes=[rc])

        units = []
        for hi, h in enumerate(heads):
            ty = head_type(h)
            for m in range(NT):
                g0 = max(0, m - 4) if ty == "dil" else 0
                for c in range(2 if ty == "diff" else 1):
                    for g in range(g0, m + 1):
                        units.append(dict(hi=hi, h=h, ty=ty, m=m, c=c, g=g, first=(g == g0), last=(g == m)))
        nU = len(units)
        state = dict(loaded=-1)

        def load_head(hi):
            h = heads[hi]; b = hi % 2
            w = [rhd[b]]
            kb.dma("sp", s_hd[b], kt[b][:], kT[h], writes=w)
            kb.dma("sp", s_hd[b], va[b][:], va_d[h].rearrange("p (t d) -> p t d", d=130), writes=w)
            kb.dma("sp", s_hd[b], qt[b][:], qT[h], writes=w)
            kb.dma("sp", s_hd[b], bias[b][:], bt_d[h], writes=w)
            ci = 1 if head_type(h) == "dil" else 0
            kb.op("pool", lambda e, b=b, ci=ci: e.tensor_tensor(bias[b][:], bias[b][:], cm[:, ci, :], ALU.add), reads=[rhd[b], rc], writes=[rhd[b]])

        def moba_sel(u):
            b = u["hi"] % 2; m = u["m"]; sb_ = m % 2
            qs = qt[b][:, m * 128:(m + 1) * 128]
            kb.op("pe", lambda e: e.matmul(gps[:, 0:32], qs, kmh[:], start=True, stop=False), reads=[rhd[b], rkm], writes=[rgps])
            kb.op("pe", lambda e: e.matmul(gps[:, 0:32], qs, kml[:], start=False, stop=True), reads=[rhd[b], rkm], writes=[rgps])
            G = gm[sb_]; R = rgm[sb_]
            kb.op(V, lambda e: e.tensor_tensor(G[:], gps[:, 0:32], pm[:, m, 0:32], ALU.add), reads=[rgps, rc], writes=[R])
            kb.op(V, lambda e: e.max(m8[sb_][:], G[:]), reads=[R], writes=[R])
            kb.op(V, lambda e: e.tensor_scalar(G[:], G[:], m8[sb_][:, 2:3], None, ALU.is_ge), reads=[R], writes=[R])
            kb.op(V, lambda e: e.tensor_tensor(G[:], G[:], pm[:, m, 32:64], ALU.mult), reads=[R, rc], writes=[R])
            kb.op(V, lambda e: e.tensor_tensor(G[:], G[:], pm[:, m, 64:96], ALU.add), reads=[R, rc], writes=[R])
            kb.op(V, lambda e: e.tensor_scalar(G[:], G[:], -NEG, NEG, ALU.mult, ALU.add), reads=[R], writes=[R])
            kb.op("pe", lambda e: e.transpose(gps[0:32, 128:256], G[:], identf[:]), reads=[R, rc], writes=[rgps])
            kb.op("act", lambda e: e.activation(selT[sb_][:], gps[0:32, 128:256], AF.Copy), reads=[rgps], writes=[rselT[sb_]])

        def qk(ui):
            u = units[ui]; hi = u["hi"]; b = hi % 2; ty = u["ty"]; m = u["m"]; g = u["g"]; c = u["c"]
            if hi > state["loaded"]:
                if state["loaded"] < 0:
                    load_head(0)
                state["loaded"] = hi
                if ty == "moba":
                    kb.op(V, lambda e, b=b: e.tensor_reduce(km[:], kt[b][:].rearrange("p (n k) -> p n k", k=256), AX.X, ALU.add), reads=[rhd[b]], writes=[rkm])
                    kb.op(V, lambda e: e.tensor_copy(kmh[:], km[:]), reads=[rkm], writes=[rkm])
                    kb.op(V, lambda e: e.tensor_tensor(km[:], km[:], kmh[:], ALU.subtract), reads=[rkm], writes=[rkm])
                    kb.op(V, lambda e: e.tensor_copy(kml[:], km[:]), reads=[rkm], writes=[rkm])
            if ty == "moba" and u["first"]:
                moba_sel(u)
            sp_ = ui % 3
            for i in range(4):
                kt_i = 4 * g + i
                o = sps[sp_][:, i * 128:(i + 1) * 128]
                if ty == "diff":
                    lh = kt[b][64 * c:64 * c + 64, kt_i * 128:(kt_i + 1) * 128]
                    rh = qt[b][64 * c:64 * c + 64, m * 128:(m + 1) * 128]
                else:
                    lh = kt[b][:, kt_i * 128:(kt_i + 1) * 128]
                    rh = qt[b][:, m * 128:(m + 1) * 128]
                mo = (ty == "moba")
                kb.op("pe", lambda e, o=o, lh=lh, rh=rh, mo=mo: e.matmul(o, lh, rh, start=True, stop=(not mo)), reads=[rhd[b]], writes=[rsps[sp_]])
                if mo:
                    n = 2 * g + i // 2
                    kb.op("pe", lambda e, o=o, n=n, sb_=m % 2: e.matmul(o, esel[:, n, :], selT[sb_][:], start=False, stop=True), reads=[rc, rselT[m % 2]], writes=[rsps[sp_]])
            idx = min(m - g, 5)
            kb.op(V, lambda e: e.tensor_tensor(tmp[sp_][:], sps[sp_][:], bias[b][:, idx * 512:(idx + 1) * 512], ALU.add), reads=[rsps[sp_], rhd[b]], writes=[rtmp[sp_]])
            kb.op("act", lambda e: e.activation(pT[sp_][:], tmp[sp_][:], AF.Exp), reads=[rtmp[sp_]], writes=[rpT[sp_]])

        qslot_ctr = dict(n=0)

        def pv(ui):
            u = units[ui]; hi = u["hi"]; b = hi % 2; ty = u["ty"]; m = u["m"]; g = u["g"]; c = u["c"]
            sp_ = ui % 3
            oi = (m % 2) * 2 + c
            if m == 0 and c == 0 and u["first"] and hi + 1 < len(heads):
                load_head(hi + 1)
            for i in range(4):
                kt_i = 4 * g + i
                kb.op("pe", lambda e, i=i, kt_i=kt_i: e.matmul(ops_[oi][:, 0:129], pT[sp_][:, i * 128:(i + 1) * 128], va[b][:, kt_i, 0:129], start=(u["first"] and i == 0), stop=(u["last"] and i == 3)),
                      reads=[rpT[sp_], rhd[b]], writes=[rops[oi]])
            if not u["last"]:
                return
            if ty == "diff" and c == 0:
                return
            ob = hi % 2
            dst = ost[ob][:, m, :]
            R = rs[m % 2]; RR = rrs[m % 2]
            if ty != "diff":
                kb.op(V, lambda e: e.reciprocal(R[:, 0:1], ops_[oi][:, 128:129]), reads=[rops[oi]], writes=[RR])
                kb.op(V, lambda e: e.tensor_scalar(dst, ops_[oi][:, 0:128], R[:, 0:1], None, ALU.mult), reads=[rops[oi], RR], writes=[rost[ob]])
            else:
                o0 = ops_[(m % 2) * 2]; o1 = ops_[(m % 2) * 2 + 1]; r0 = rops[(m % 2) * 2]; r1 = rops[(m % 2) * 2 + 1]
                F = of[m % 2]; RF = rof[m % 2]
                kb.op(V, lambda e: e.reciprocal(R[:, 0:1], o0[:, 128:129]), reads=[r0], writes=[RR])
                kb.op(V, lambda e: e.reciprocal(R[:, 1:2], o1[:, 128:129]), reads=[r1], writes=[RR])
                kb.op(V, lambda e: e.tensor_tensor(R[:, 1:2], R[:, 1:2], sm[:, 4:5], ALU.mult), reads=[RR, rsm], writes=[RR])
                kb.op(V, lambda e: e.tensor_scalar(F[:], o0[:, 0:128], R[:, 0:1], None, ALU.mult), reads=[r0, RR], writes=[RF])
                kb.op(V, lambda e: e.scalar_tensor_tensor(F[:], o1[:, 0:128], R[:, 1:2], F[:], ALU.mult, ALU.add), reads=[r1, RR, RF], writes=[RF])
                kb.op("act", lambda e: e.activation(junk[:], F[:], AF.Square, accum_out=R[:, 2:3]), reads=[RF], writes=[rjunk, RR])
                kb.op(V, lambda e: e.tensor_scalar(R[:, 3:4], R[:, 2:3], 1.0 / 128, 1e-6, ALU.mult, ALU.add), reads=[RR], writes=[RR])
                kb.op("act", lambda e: e.activation(R[:, 4:5], R[:, 3:4], AF.Sqrt), reads=[RR], writes=[RR])
                kb.op(V, lambda e: e.reciprocal(R[:, 5:6], R[:, 4:5]), reads=[RR], writes=[RR])
                kb.op(V, lambda e: e.scalar_tensor_tensor(dst, F[:], R[:, 5:6], sg[:], ALU.mult, ALU.mult), reads=[RF, RR, rc], writes=[rost[ob]])
            if m == NT - 1:
                kb.dma("sp", s_ost[ob], out_d[u["h"]], ost[ob][:].rearrange("p m d -> p (m d)"), reads=[rost[ob]])

        LA = 2
        for ui in range(min(LA, nU)):
            qk(ui)
        for ui in range(nU):
            if ui + LA < nU:
                qk(ui + LA)
            pv(ui)
        for s in s_ost:
            kb.wait_slot("sp", s)
        kb.emit()
    return nc


D = 2048
DFF = 5632
NFC = 44
RT = [(0, 2), (2, 128), (130, 128), (258, 128), (386, 128)]
SEG = [(0, 512), (512, 2)]

def build_C(final=False):
    nc = bass.Bass("TRN2", target_bir_lowering=False)
    din = lambda n, s, d: nc.dram_tensor(n, s, d, kind="ExternalInput").ap()
    xin = din("xin", [2050, D], F32)
    mixT = din("mixT", [128, 16, 2050], BF16)
    w_out = din("w_out", [D, D], F32); w_cq = din("w_cq", [D, 512], F32); w_ckv = din("w_ckv", [D, 1024], F32)
    w_co = din("w_co", [512, D], F32); w_up = din("w_up", [D, 2 * DFF], F32); w_down = din("w_down", [DFF, D], F32)
    g_cross = din("g_cross", [1, D], F32); g_mem = din("g_mem", [1, D], F32); g_ffn = din("g_ffn", [1, D], F32)
    g_fin = din("g_final", [1, D], F32)
    mem = din("mem", [256, D], F32)
    cwh = din("cwh", [128, NFC * 4], F32)
    hmask = din("hmask", [1, 1], F32)
    ident_d = din("ident", [128, 128], BF16)
    xo = nc.dram_tensor("xo", [2048, D], F32, kind="ExternalOutput").ap()
    with ExitStack() as st:
        kb = KB(nc, st)
        sb = lambda n, s, d: st.enter_context(nc.sbuf_tensor("s_" + n, s, d))
        ps = lambda n, s, d: st.enter_context(nc.psum_tensor("p_" + n, s, d))
        x1q = sb("x1q", [128, 5, D], F32); rx1 = [Res() for _ in range(5)]
        hT = sb("hT", [128, 16, 514], BF16); rhT = Res()
        qcT = sb("qcT", [128, 4, 514], BF16); rqc = Res()
        ocT = sb("ocT", [128, 4, 514], BF16); roc = Res()
        aT = sb("aT", [128, NFC, 512], BF16); raT = Res()
        wblk = [sb("wblk%d" % i, [128, 8192], BF16) for i in range(2)]; rwb = [Res(), Res()]
        s_w = [kb.slot("w0"), kb.slot("w1")]
        kcT = sb("kcT", [128, 4, 256], BF16); vc = sb("vc", [128, 2, 4, 130], BF16); rkv = Res()
        gt = sb("gt", [128, D], F32); rg = Res(); s_g = kb.slot("g")
        ident = sb("ident", [128, 128], BF16); rid = Res()
        hb = sb("hb", [128, D], BF16); rhb = Res()
        junk = sb("junk", [128, D], BF16); rj = Res()
        small = sb("small", [128, 4], F32); rsm = Res()
        cw = sb("cw", [128, NFC, 4], F32); hm = sb("hm", [128, 1], F32); rcw = Res()
        G = [sb("G%d" % i, [128, 514], F32) for i in range(2)]; rG = [Res(), Res()]
        T1 = [sb("T1%d" % i, [128, 512], F32) for i in range(2)]; rT1 = [Res(), Res()]
        Sg = [sb("Sg%d" % i, [128, 512], F32) for i in range(2)]; rSg = [Res(), Res()]
        pT = [sb("pT%d" % i, [128, 2, 128], BF16) for i in range(2)]; rpT = [Res(), Res()]
        oc = [sb("oc%d" % i, [128, 512], BF16) for i in range(2)]; roct = [Res(), Res()]
        rr = sb("rr", [128, 2], F32); rrr = Res()
        tp = [ps("tp%d" % i, [128, 512], BF16) for i in range(2)]; rtp = [Res(), Res()]
        mm = [ps("mm%d" % i, [128, 512], F32) for i in range(4)]; rmm = [Res() for _ in range(4)]
        m2 = [ps("m2%d" % i, [128, 512], F32) for i in range(2)]; rm2 = [Res(), Res()]
        s_c = kb.slot("const"); s_x = kb.slot("x"); s_o = kb.slot("o")
        cnt = dict(w=0, mm=0, tp=0, m2=0, ev=0)
        kb.dma("sp", s_c, ident[:], ident_d, writes=[rid])
        kb.dma("sp", s_c, cw[:], cwh.rearrange("p (f k) -> p f k", k=4), writes=[rcw])
        kb.dma("sp", s_c, hm[:], hmask.broadcast_to([128, 1]), writes=[rcw])
        kb.op("pool", lambda e: e.memset(vc[:], 1.0), writes=[rkv])

        def load_g(gd):
            kb.dma("sp", s_g, gt[:], gd.broadcast_to([128, D]), writes=[rg])

        def load_w(src, nk, ncol):
            b = cnt["w"] % 2; cnt["w"] += 1
            view = wblk[b][:, 0:nk * ncol].rearrange("p (k n) -> p k n", n=ncol)
            sv = src.rearrange("(kc kp) n -> kp kc n", kp=128)
            step = max(1, 2048 // ncol)
            for k0 in range(0, nk, step):
                k1 = min(nk, k0 + step)
                kb.dma("pool", s_w[b], view[:, k0:k1, :], sv[:, k0:k1, :], writes=[rwb[b]])
            return view, rwb[b]

        def evac(dst, src, reads, writes, scale=None):
            cnt["ev"] += 1
            if cnt["ev"] % 2 == 0:
                if scale is None:
                    kb.op("act", lambda e: e.activation(dst, src, AF.Copy), reads=reads, writes=writes)
                else:
                    kb.op("act", lambda e: e.activation(dst, src, AF.Copy, scale=scale), reads=reads, writes=writes)
            else:
                if scale is None:
                    kb.op("dve", lambda e: e.tensor_copy(dst, src), reads=reads, writes=writes)
                else:
                    kb.op("dve", lambda e: e.tensor_scalar(dst, src, scale, None, ALU.mult), reads=reads, writes=writes)

        def norm_T(src_tile, rsrc, nr, dstT, rdst, col0):
            ss = small[0:nr, 0:1]; ms = small[0:nr, 1:2]; sd = small[0:nr, 2:3]; rstd = small[0:nr, 3:4]
            kb.op("act", lambda e: e.activation(junk[0:nr, :], src_tile, AF.Square, accum_out=ss), reads=[rsrc], writes=[rj, rsm])
            kb.op("dve", lambda e: e.tensor_scalar(ms, ss, 1.0 / D, 1e-6, ALU.mult, ALU.add), reads=[rsm], writes=[rsm])
            kb.op("act", lambda e: e.activation(sd, ms, AF.Sqrt), reads=[rsm], writes=[rsm])
            kb.op("dve", lambda e: e.reciprocal(rstd, sd), reads=[rsm], writes=[rsm])
            kb.op("dve", lambda e: e.scalar_tensor_tensor(hb[0:nr, :], src_tile, rstd, gt[0:nr, :], ALU.mult, ALU.mult), reads=[rsrc, rsm, rg], writes=[rhb])
            for c4 in range(4):
                pb = cnt["tp"] % 2; cnt["tp"] += 1
                for i in range(4):
                    c = c4 * 4 + i
                    kb.op("pe", lambda e, i=i, c=c: e.transpose(tp[pb][:, i * 128:i * 128 + nr], hb[0:nr, c * 128:(c + 1) * 128], ident[0:nr, 0:nr]), reads=[rhb, rid], writes=[rtp[pb]])
                src = tp[pb][:].rearrange("p (c t) -> p c t", c=4)[:, :, 0:nr]
                dst = dstT[:, c4 * 4:(c4 + 1) * 4, col0:col0 + nr]
                evac(dst, src, [rtp[pb]], [rdst])

        def newmm():
            i = cnt["mm"] % 4; cnt["mm"] += 1
            return mm[i], rmm[i]

        load_g(g_mem)
        for t in range(2):
            kb.dma("sp", s_x, x1q[:, t, :], mem[t * 128:(t + 1) * 128, :], writes=[rx1[t]])
            norm_T(x1q[:, t, :], rx1[t], 128, hT, rhT, t * 128)
        for half in range(2):
            wv, rw = load_w(w_ckv[:, half * 512:(half + 1) * 512], 16, 512)
            if half == 0:
                for hd in range(4):
                    P, rP = newmm()
                    for kc in range(16):
                        kb.op("pe", lambda e, kc=kc, hd=hd: e.matmul(P[:, 0:256], wv[:, kc, hd * 128:(hd + 1) * 128], hT[:, kc, 0:256], start=(kc == 0), stop=(kc == 15)), reads=[rw, rhT], writes=[rP])
                    evac(kcT[:, hd, :], P[:, 0:256], [rP], [rkv])
            else:
                for t in range(2):
                    P, rP = newmm()
                    for kc in range(16):
                        kb.op("pe", lambda e, kc=kc, t=t: e.matmul(P[:], hT[:, kc, t * 128:(t + 1) * 128], wv[:, kc, :], start=(kc == 0), stop=(kc == 15)), reads=[rw, rhT], writes=[rP])
                    evac(vc[:, t, :, 0:128], P[:].rearrange("p (h d) -> p h d", h=4), [rP], [rkv])

        for q in range(4):
            R0 = 512 * q
            for ti, (r0, nr) in enumerate(RT):
                kb.dma("sp", s_x, x1q[0:nr, ti, :], xin[R0 + r0:R0 + r0 + nr, :], writes=[rx1[ti]])
            kb.dma("sp", s_x, hT[:], mixT[:, :, R0:R0 + 514], writes=[rhT])
            for cbk in range(4):
                wv, rw = load_w(w_out[:, cbk * 512:(cbk + 1) * 512], 16, 512)
                for ti, (r0, nr) in enumerate(RT):
                    P, rP = newmm()
                    for kc in range(16):
                        kb.op("pe", lambda e, kc=kc, r0=r0, nr=nr: e.matmul(P[0:nr, :], hT[:, kc, r0:r0 + nr], wv[:, kc, :], start=(kc == 0), stop=(kc == 15)), reads=[rw, rhT], writes=[rP])
                    dst = x1q[0:nr, ti, cbk * 512:(cbk + 1) * 512]
                    kb.op("dve", lambda e, dst=dst, nr=nr: e.tensor_tensor(dst, P[0:nr, :], dst, ALU.add), reads=[rP, rx1[ti]], writes=[rx1[ti]])
            load_g(g_cross)
            for ti, (r0, nr) in enumerate(RT):
                norm_T(x1q[0:nr, ti, :], rx1[ti], nr, hT, rhT, r0)
            wv, rw = load_w(w_cq, 16, 512)
            for hd in range(4):
                for (c0, n) in SEG:
                    P, rP = newmm()
                    for kc in range(16):
                        kb.op("pe", lambda e, kc=kc, hd=hd, c0=c0, n=n: e.matmul(P[:, 0:n], wv[:, kc, hd * 128:(hd + 1) * 128], hT[:, kc, c0:c0 + n], start=(kc == 0), stop=(kc == 15)), reads=[rw, rhT], writes=[rP])
                    evac(qcT[:, hd, c0:c0 + n], P[:, 0:n], [rP], [rqc], scale=128 ** -0.5)
            for ti, (r0, nr) in enumerate(RT):
                ob = ti % 2
                for hd in range(4):
                    P, rP = newmm()
                    for mt in range(2):
                        kb.op("pe", lambda e, mt=mt, hd=hd, r0=r0, nr=nr: e.matmul(P[:, mt * 128:mt * 128 + nr], kcT[:, hd, mt * 128:(mt + 1) * 128], qcT[:, hd, r0:r0 + nr], start=True, stop=True), reads=[rkv, rqc], writes=[rP])
                    pb = cnt["m2"] % 2; cnt["m2"] += 1
                    kb.op("act", lambda e, pb=pb, nr=nr: e.activation(pT[pb][:, :, 0:nr], P[:, 0:256].rearrange("p (m r) -> p m r", m=2)[:, :, 0:nr], AF.Exp), reads=[rP], writes=[rpT[pb]])
                    O, rO = m2[pb], rm2[pb]
                    for mt in range(2):
                        kb.op("pe", lambda e, mt=mt, hd=hd, pb=pb, nr=nr: e.matmul(O[0:nr, 0:129], pT[pb][:, mt, 0:nr], vc[:, mt, hd, 0:129], start=(mt == 0), stop=(mt == 1)), reads=[rpT[pb], rkv], writes=[rO])
                    kb.op("dve", lambda e, nr=nr: e.reciprocal(rr[0:nr, 0:1], O[0:nr, 128:129]), reads=[rO], writes=[rrr])
                    kb.op("dve", lambda e, nr=nr, hd=hd, ob=ob: e.tensor_scalar(oc[ob][0:nr, hd * 128:(hd + 1) * 128], O[0:nr, 0:128], rr[0:nr, 0:1], None, ALU.mult), reads=[rO, rrr], writes=[roct[ob]])
                pb = cnt["tp"] % 2; cnt["tp"] += 1
                for hd in range(4):
                    kb.op("pe", lambda e, hd=hd, nr=nr, ob=ob: e.transpose(tp[pb][:, hd * 128:hd * 128 + nr], oc[ob][0:nr, hd * 128:(hd + 1) * 128], ident[0:nr, 0:nr]), reads=[roct[ob], rid], writes=[rtp[pb]])
                evac(ocT[:, :, r0:r0 + nr], tp[pb][:].rearrange("p (c t) -> p c t", c=4)[:, :, 0:nr], [rtp[pb]], [roc])
            wv, rw = load_w(w_co, 4, 2048)
            for cbk in range(4):
                for ti, (r0, nr) in enumerate(RT):
                    P, rP = newmm()
                    for kc in range(4):
                        kb.op("pe", lambda e, kc=kc, r0=r0, nr=nr, cbk=cbk: e.matmul(P[0:nr, :], ocT[:, kc, r0:r0 + nr], wv[:, kc, cbk * 512:(cbk + 1) * 512], start=(kc == 0), stop=(kc == 3)), reads=[rw, roc], writes=[rP])
                    dst = x1q[0:nr, ti, cbk * 512:(cbk + 1) * 512]
                    kb.op("dve", lambda e, dst=dst, nr=nr: e.tensor_tensor(dst, P[0:nr, :], dst, ALU.add), reads=[rP, rx1[ti]], writes=[rx1[ti]])
            load_g(g_ffn)
            for ti, (r0, nr) in enumerate(RT):
                norm_T(x1q[0:nr, ti, :], rx1[ti], nr, hT, rhT, r0)
            for fb in range(22):
                b = cnt["w"] % 2; cnt["w"] += 1
                wv = wblk[b][:, 0:8192].rearrange("p (k n) -> p k n", n=512); rw = rwb[b]
                for (col0, dcol) in ((fb * 256, 0), (DFF + fb * 256, 256)):
                    sv = w_up[:, col0:col0 + 256].rearrange("(kc kp) n -> kp kc n", kp=128)
                    for k0 in range(0, 16, 8):
                        kb.dma("pool", s_w[b], wv[:, k0:k0 + 8, dcol:dcol + 256], sv[:, k0:k0 + 8, :], writes=[rw])
                for f2 in range(2):
                    fc = fb * 2 + f2
                    gb = fc % 2
                    for (c0, n) in SEG:
                        P, rP = newmm()
                        for kc in range(16):
                            kb.op("pe", lambda e, kc=kc, f2=f2, c0=c0, n=n: e.matmul(P[:, 0:n], wv[:, kc, f2 * 128:(f2 + 1) * 128], hT[:, kc, c0:c0 + n], start=(kc == 0), stop=(kc == 15)), reads=[rw, rhT], writes=[rP])
                        kb.op("act", lambda e, c0=c0, n=n, gb=gb: e.activation(G[gb][:, c0:c0 + n], P[:, 0:n], AF.Copy), reads=[rP], writes=[rG[gb]])
                    if q == 0:
                        kb.op("dve", lambda e, gb=gb: e.tensor_scalar(G[gb][:, 0:2], G[gb][:, 0:2], hm[:, 0:1], None, ALU.mult), reads=[rG[gb], rcw], writes=[rG[gb]])
                    kb.op("dve", lambda e, gb=gb, fc=fc: e.tensor_scalar(T1[gb][:], G[gb][:, 0:512], cw[:, fc, 0:1], cw[:, fc, 3:4], ALU.mult, ALU.add), reads=[rG[gb], rcw], writes=[rT1[gb]])
                    kb.op("dve", lambda e, gb=gb, fc=fc: e.scalar_tensor_tensor(T1[gb][:], G[gb][:, 1:513], cw[:, fc, 1:2], T1[gb][:], ALU.mult, ALU.add), reads=[rG[gb], rcw, rT1[gb]], writes=[rT1[gb]])
                    kb.op("dve", lambda e, gb=gb, fc=fc: e.scalar_tensor_tensor(T1[gb][:], G[gb][:, 2:514], cw[:, fc, 2:3], T1[gb][:], ALU.mult, ALU.add), reads=[rG[gb], rcw, rT1[gb]], writes=[rT1[gb]])
                    kb.op("act", lambda e, gb=gb: e.activation(Sg[gb][:], T1[gb][:], AF.Silu), reads=[rT1[gb]], writes=[rSg[gb]])
                    P, rP = newmm()
                    for kc in range(16):
                        kb.op("pe", lambda e, kc=kc, f2=f2: e.matmul(P[:], wv[:, kc, 256 + f2 * 128:256 + (f2 + 1) * 128], hT[:, kc, 2:514], start=(kc == 0), stop=(kc == 15)), reads=[rw, rhT], writes=[rP])
                    kb.op("dve", lambda e, gb=gb, fc=fc: e.tensor_tensor(aT[:, fc, :], P[:], Sg[gb][:], ALU.mult), reads=[rP, rSg[gb]], writes=[raT])
            if final:
                load_g(g_fin)
            for cbk in range(16):
                b = cnt["w"] % 2; cnt["w"] += 1
                wv = wblk[b][:, 0:NFC * 128].rearrange("p (k n) -> p k n", n=128); rw = rwb[b]
                sv = w_down[:, cbk * 128:(cbk + 1) * 128].rearrange("(kc kp) n -> kp kc n", kp=128)
                for k0 in range(0, NFC, 11):
                    kb.dma("pool", s_w[b], wv[:, k0:k0 + 11, :], sv[:, k0:k0 + 11, :], writes=[rw])
                for ti in range(1, 5):
                    P, rP = newmm()
                    for fc in range(NFC):
                        kb.op("pe", lambda e, fc=fc, ti=ti: e.matmul(P[:, 0:128], aT[:, fc, (ti - 1) * 128:ti * 128], wv[:, fc, :], start=(fc == 0), stop=(fc == NFC - 1)), reads=[rw, raT], writes=[rP])
                    dst = x1q[:, ti, cbk * 128:(cbk + 1) * 128]
                    kb.op("dve", lambda e, dst=dst: e.tensor_tensor(dst, P[:, 0:128], dst, ALU.add), reads=[rP, rx1[ti]], writes=[rx1[ti]])
            for ti in range(1, 5):
                if final:
                    ss = small[:, 0:1]; ms = small[:, 1:2]; sd = small[:, 2:3]; rstd = small[:, 3:4]
                    src = x1q[:, ti, :]
                    kb.op("act", lambda e, src=src: e.activation(junk[:], src, AF.Square, accum_out=ss), reads=[rx1[ti]], writes=[rj, rsm])
                    kb.op("dve", lambda e: e.tensor_scalar(ms, ss, 1.0 / D, 1e-6, ALU.mult, ALU.add), reads=[rsm], writes=[rsm])
                    kb.op("act", lambda e: e.activation(sd, ms, AF.Sqrt), reads=[rsm], writes=[rsm])
                    kb.op("dve", lambda e: e.reciprocal(rstd, sd), reads=[rsm], writes=[rsm])
                    kb.op("dve", lambda e, src=src: e.scalar_tensor_tensor(src, src, rstd, gt[:], ALU.mult, ALU.mult), reads=[rsm, rg, rx1[ti]], writes=[rx1[ti]])
                kb.dma("sp", s_o, xo[R0 + (ti - 1) * 128:R0 + ti * 128, :], x1q[:, ti, :], reads=[rx1[ti]])
        kb.wait_slot("sp", s_o)
        kb.emit()
    return nc


_PROGS = {}

def _prog(name):
    if name not in _PROGS:
        if name == "A":
            _PROGS[name] = build_A()
        elif name == "B":
            _PROGS[name] = build_B()
        elif name == "C":
            _PROGS[name] = build_C(False)
        else:
            _PROGS[name] = build_C(True)
    return _PROGS[name]


def kernel(x, mem, w_in, w_out, g_mix, diff_lambda, diff_subln, rel_bias_table,
           g_cross, g_mem, w_cq, w_ckv, w_co, g_ffn, w_up, conv_w, conv_b, w_down, g_final):
    f32 = np.float32
    x = np.asarray(x, f32); mem = np.asarray(mem, f32)
    table = np.asarray(rel_bias_table, f32)
    ident_bf = np.eye(128, dtype=f32).astype(NPBF)
    ident_f = np.eye(128, dtype=f32)
    esel = esel_const()
    bts = [bias_tables(table, j) for j in range(4)]
    cms = [const_masks(j) for j in range(4)]
    pms = [moba_consts(j) for j in range(4)]
    cores = list(range(8))
    cur = x
    for l in range(4):
        maps = []
        for c in cores:
            b, j = c // 4, c % 4
            xo = np.ascontiguousarray(cur[b].reshape(64, 128, 2048)[j::4].reshape(2048, 2048))
            maps.append(dict(x=xo, w_in=np.asarray(w_in[l], f32), g_mix=np.asarray(g_mix[l], f32).reshape(1, -1), ident=ident_bf))
        rA = run_bass_kernel_spmd(_prog("A"), maps, core_ids=cores).results
        kT_full = np.zeros((2, 16, 128, 64, 128), NPBF)
        vT_full = np.zeros((2, 16, 128, 64, 128), NPBF)
        for c in cores:
            b, j = c // 4, c % 4
            kT_full[b][:, :, j::4, :] = np.asarray(rA[c]["kT"]).reshape(16, 128, 16, 128)
            vT_full[b][:, :, j::4, :] = np.asarray(rA[c]["vT"]).reshape(16, 128, 16, 128)
        vas = []
        for b in range(2):
            va = np.zeros((16, 128, 64, 130), NPBF)
            va[:, :, :, :128] = vT_full[b].transpose(0, 3, 2, 1)
            va[:, :, :, 128] = 1.0
            vas.append(va.reshape(16, 128, 64 * 130))
        lam_init = 0.8 - 0.6 * math.exp(-0.3 * l)
        maps = []
        for c in cores:
            b, j = c // 4, c % 4
            maps.append(dict(qT=np.asarray(rA[c]["qT"]), kT=kT_full[b].reshape(16, 128, 8192), va=vas[b], bt=bts[j], cm=cms[j], pm=pms[j],
                             esel=esel, identf=ident_f, dl=np.asarray(diff_lambda[l], f32).reshape(1, 256),
                             subln=np.asarray(diff_subln[l], f32).reshape(1, 128), laminit=np.array([[lam_init]], f32)))
        rB = run_bass_kernel_spmd(_prog("B"), maps, core_ids=cores).results
        del rA, kT_full, vT_full, vas
        mixed = np.zeros((2, 64, 128, 16, 128), NPBF)
        for c in cores:
            b, j = c // 4, c % 4
            mixed[b][j::4] = np.asarray(rB[c]["mixed"]).reshape(16, 128, 16, 128).transpose(2, 1, 0, 3)
        mixed = mixed.reshape(2, 8192, 2048)
        cwh = np.zeros((128, 44, 4), f32)
        cwh[:, :, 0:3] = np.asarray(conv_w[l], f32).reshape(3, 44, 128).transpose(2, 1, 0)
        cwh[:, :, 3] = np.asarray(conv_b[l], f32).reshape(44, 128).T
        cwh = cwh.reshape(128, 44 * 4)
        maps = []
        for c in cores:
            b, j = c // 4, c % 4
            xin = np.zeros((2050, 2048), f32)
            mrow = np.zeros((2050, 2048), NPBF)
            lo = 2048 * j
            xin[2:] = cur[b][lo:lo + 2048]
            mrow[2:] = mixed[b][lo:lo + 2048]
            if j > 0:
                xin[0:2] = cur[b][lo - 2:lo]
                mrow[0:2] = mixed[b][lo - 2:lo]
            mixT = np.ascontiguousarray(mrow.reshape(2050, 16, 128).transpose(2, 1, 0))
            maps.append(dict(xin=xin, mixT=mixT, w_out=np.asarray(w_out[l], f32), w_cq=np.asarray(w_cq[l], f32), w_ckv=np.asarray(w_ckv[l], f32),
                             w_co=np.asarray(w_co[l], f32), w_up=np.asarray(w_up[l], f32), w_down=np.asarray(w_down[l], f32),
                             g_cross=np.asarray(g_cross[l], f32).reshape(1, -1), g_mem=np.asarray(g_mem[l], f32).reshape(1, -1),
                             g_ffn=np.asarray(g_ffn[l], f32).reshape(1, -1), g_final=np.asarray(g_final, f32).reshape(1, -1),
                             mem=mem[b], cwh=cwh, hmask=np.array([[0.0 if j == 0 else 1.0]], f32), ident=ident_bf))
        rC = run_bass_kernel_spmd(_prog("Cf" if l == 3 else "C"), maps, core_ids=cores).results
        nxt = np.zeros((2, 8192, 2048), f32)
        for c in cores:
            b, j = c // 4, c % 4
            nxt[b, 2048 * j:2048 * j + 2048] = np.asarray(rC[c]["xo"])
        cur = nxt
    return cur.astype(np.float32)
```
